# Optimizing a Trainium2 kernel written in Bass

```python
import jax, jax.numpy as jnp
from jax import lax
import numpy as np

D_MODEL = 1024
BATCH = 1
SEQ = 16384
DEPTH = 4

N_MIXERS = 3
PLE_DIM = 256
D_FF = 2816
EPS = 1e-6
HG_HEADS = 8
HG_DK = D_MODEL // HG_HEADS
HG_DV = D_MODEL // HG_HEADS
HG_CHUNK = 64
ML_HEADS = 4
ML_DQK = D_MODEL // 2 // ML_HEADS
ML_DV = D_MODEL // ML_HEADS
ML_CHUNK = 64
ML_GATE_CAP = 15.0
SW_HEADS = 16
SW_KV_HEADS = 4
SW_GROUP = SW_HEADS // SW_KV_HEADS
SW_HD = 64
WINDOW = 128
SW_BLOCK = 128
N_A = (DEPTH + 2) // 3
N_B = (DEPTH + 1) // 3
N_C = DEPTH // 3

kernel_name = "hybrid_hgrn2_mlstm_swa_macaron"


def rmsnorm(x, gain):
    xf = x.astype(jnp.float32)
    y = xf * lax.rsqrt(jnp.mean(xf * xf, axis=-1, keepdims=True) + EPS)
    return (y * gain.astype(jnp.float32)).astype(x.dtype)


def swiglu(x, w_gu, w_down):
    g, u = jnp.split(x @ w_gu, 2, axis=-1)
    return (jax.nn.silu(g) * u) @ w_down


def to_chunks(t, L):
    B, T, H, d = t.shape
    return t.reshape(B, T // L, L, H, d).transpose(1, 0, 3, 2, 4)


def from_chunks(t):
    N, B, H, L, d = t.shape
    return t.transpose(1, 0, 3, 2, 4).reshape(B, N * L, H, d)


def gate_chunks(t, L):
    B, T, H = t.shape
    return t.reshape(B, T // L, L, H).transpose(1, 0, 3, 2)


def alibi_slopes(n_heads):
    return jnp.asarray([2.0 ** (-8.0 * (h + 1) / n_heads) for h in range(n_heads)], jnp.float32)


def hgrn2_mixer(x, lb, w_in, g_norm, w_out):
    B, T, _ = x.shape
    q, f, i, og = jnp.split(x @ w_in, 4, axis=-1)
    q = jax.nn.silu(q).astype(jnp.float32)
    f = f.astype(jnp.float32)
    lb = lb.astype(jnp.float32)
    log_f = jnp.logaddexp(jnp.log(lb), jnp.log1p(-lb) + jax.nn.log_sigmoid(f))
    k = (1.0 - lb) * jax.nn.sigmoid(-f)
    L = HG_CHUNK
    qc = to_chunks(q.reshape(B, T, HG_HEADS, HG_DK), L)
    kc = to_chunks(k.reshape(B, T, HG_HEADS, HG_DK), L)
    gc = to_chunks(log_f.reshape(B, T, HG_HEADS, HG_DK), L)
    vc = to_chunks(i.astype(jnp.float32).reshape(B, T, HG_HEADS, HG_DV), L)
    causal = jnp.tril(jnp.ones((L, L), bool))

    def step(S, inp):
        q_, k_, v_, g_ = inp
        b = jnp.cumsum(g_, axis=2)
        o_inter = jnp.einsum('bhtk,bhkv->bhtv', q_ * jnp.exp(b), S)
        diff = b[:, :, :, None, :] - b[:, :, None, :, :]
        decay = jnp.exp(jnp.where(causal[:, :, None], diff, -jnp.inf))
        A = jnp.einsum('bhtk,bhsk,bhtsk->bhts', q_, k_, decay)
        o_intra = jnp.einsum('bhts,bhsv->bhtv', A, v_)
        b_last = b[:, :, -1:, :]
        S_new = jnp.exp(b_last[:, :, 0, :])[..., None] * S + jnp.einsum(
            'bhsk,bhsv->bhkv', k_ * jnp.exp(b_last - b), v_)
        return S_new, o_inter + o_intra

    S0 = jnp.zeros((B, HG_HEADS, HG_DK, HG_DV), jnp.float32)
    _, o = lax.scan(step, S0, (qc, kc, vc, gc))
    o = rmsnorm(from_chunks(o), g_norm)
    o = o.reshape(B, T, D_MODEL).astype(x.dtype) * jax.nn.silu(og)
    return o @ w_out


def mlstm_mixer(x, w_qkvo, w_if, b_if, norm_g, w_out):
    B, T, _ = x.shape
    dqk_all = ML_HEADS * ML_DQK
    q, k, v, og = jnp.split(x @ w_qkvo, [dqk_all, 2 * dqk_all, 2 * dqk_all + D_MODEL], axis=-1)
    gates = (x @ w_if).astype(jnp.float32) + b_if.astype(jnp.float32)
    gates = ML_GATE_CAP * jnp.tanh(gates / ML_GATE_CAP)
    ig, fg = jnp.split(gates, 2, axis=-1)
    lf = jax.nn.log_sigmoid(fg)
    L = ML_CHUNK
    qc = to_chunks(q.astype(jnp.float32).reshape(B, T, ML_HEADS, ML_DQK), L)
    kc = to_chunks(k.astype(jnp.float32).reshape(B, T, ML_HEADS, ML_DQK), L) * (ML_DQK ** -0.5)
    vc = to_chunks(v.astype(jnp.float32).reshape(B, T, ML_HEADS, ML_DV), L)
    ic = gate_chunks(ig, L)
    fc = gate_chunks(lf, L)
    causal = jnp.tril(jnp.ones((L, L), bool))

    def step(carry, inp):
        C, n, m = carry
        q_, k_, v_, i_, f_ = inp
        b = jnp.cumsum(f_, axis=-1)
        log_intra = jnp.where(causal, b[..., :, None] - b[..., None, :] + i_[..., None, :], -jnp.inf)
        log_inter = b + m[..., None]
        m_t = jnp.maximum(jnp.max(log_intra, axis=-1), log_inter)
        w_intra = jnp.exp(log_intra - m_t[..., None])
        w_inter = jnp.exp(log_inter - m_t)
        scores = jnp.einsum('bhtk,bhsk->bhts', q_, k_) * w_intra
        num = w_inter[..., None] * jnp.einsum('bhtk,bhkv->bhtv', q_, C) + jnp.einsum('bhts,bhsv->bhtv', scores, v_)
        den = w_inter * jnp.einsum('bhtk,bhk->bht', q_, n) + jnp.sum(scores, axis=-1)
        h = num / jnp.maximum(jnp.abs(den), jnp.exp(-m_t))[..., None]
        b_L = b[..., -1]
        log_state = b_L + m
        log_src = b_L[..., None] - b + i_
        m_new = jnp.maximum(log_state, jnp.max(log_src, axis=-1))
        w_src = jnp.exp(log_src - m_new[..., None])
        decay = jnp.exp(log_state - m_new)
        C_new = decay[..., None, None] * C + jnp.einsum('bhsk,bhsv->bhkv', k_ * w_src[..., None], v_)
        n_new = decay[..., None] * n + jnp.einsum('bhs,bhsk->bhk', w_src, k_)
        return (C_new, n_new, m_new), h

    carry0 = (jnp.zeros((B, ML_HEADS, ML_DQK, ML_DV), jnp.float32),
              jnp.zeros((B, ML_HEADS, ML_DQK), jnp.float32),
              jnp.zeros((B, ML_HEADS), jnp.float32))
    _, h = lax.scan(step, carry0, (qc, kc, vc, ic, fc))
    h = rmsnorm(from_chunks(h), norm_g.reshape(ML_HEADS, ML_DV))
    h = h.reshape(B, T, D_MODEL).astype(x.dtype) * jax.nn.sigmoid(og)
    return h @ w_out


def swa_mixer(x, w_qkv, q_gain, k_gain, sinks, w_o):
    B, T, _ = x.shape
    nq = SW_HEADS * SW_HD
    nk = SW_KV_HEADS * SW_HD
    q, k, v = jnp.split(x @ w_qkv, [nq, nq + nk], axis=-1)
    q = rmsnorm(q.reshape(B, T, SW_HEADS, SW_HD), q_gain)
    k = rmsnorm(k.reshape(B, T, SW_KV_HEADS, SW_HD), k_gain)
    v = v.reshape(B, T, SW_KV_HEADS, SW_HD)
    nb = T // SW_BLOCK
    qb = q.reshape(B, nb, SW_BLOCK, SW_KV_HEADS, SW_GROUP, SW_HD)

    def band(t):
        tb = t.reshape(B, nb, SW_BLOCK, SW_KV_HEADS, SW_HD)
        prev = jnp.pad(tb, ((0, 0), (1, 0), (0, 0), (0, 0), (0, 0)))[:, :-1]
        return jnp.concatenate([prev, tb], axis=2)

    kk, vv = band(k), band(v)
    s = jnp.einsum('bnqkgd,bnskd->bnkgqs', qb, kk).astype(jnp.float32) * (SW_HD ** -0.5)
    qpos = jnp.arange(SW_BLOCK)
    kpos = jnp.arange(2 * SW_BLOCK) - SW_BLOCK
    dist = (qpos[:, None] - kpos[None, :]).astype(jnp.float32)
    blk_start = jnp.arange(nb) * SW_BLOCK
    valid = (dist >= 0) & (dist < WINDOW) & ((blk_start[:, None, None] + kpos[None, None, :]) >= 0)
    slopes = alibi_slopes(SW_HEADS).reshape(SW_KV_HEADS, SW_GROUP)
    s = s - slopes[:, :, None, None] * dist
    s = jnp.where(valid[None, :, None, None], s, -jnp.inf)
    sink = jnp.broadcast_to(sinks.astype(jnp.float32).reshape(SW_KV_HEADS, SW_GROUP)[:, :, None, None],
                            s.shape[:-1] + (1,))
    probs = jax.nn.softmax(jnp.concatenate([s, sink], axis=-1), axis=-1)[..., :-1]
    o = jnp.einsum('bnkgqs,bnskd->bnqkgd', probs.astype(vv.dtype), vv).reshape(B, T, nq)
    return o @ w_o


def setup_inputs(seed: int = 0) -> dict:
    key = jax.random.key(seed)
    ks = jax.random.split(key, 24)
    f32 = jnp.float32

    def nrm(k, shape, fan_in):
        return jax.random.normal(k, shape, f32) * (fan_in ** -0.5)

    D = D_MODEL
    b_if = jnp.concatenate([
        0.1 * jax.random.normal(ks[11], (N_B, ML_HEADS), f32),
        jnp.linspace(3.0, 6.0, ML_HEADS, dtype=f32)[None, :] + 0.1 * jax.random.normal(ks[12], (N_B, ML_HEADS), f32),
    ], axis=-1)
    return {
        "x": jax.random.normal(ks[0], (BATCH, SEQ, D), f32),
        "p": jax.random.normal(ks[1], (DEPTH, BATCH, SEQ, PLE_DIM), f32),
        "norm_gains": 1.0 + 0.02 * jax.random.normal(ks[2], (DEPTH, 4, D), f32),
        "w_ffn_gu": nrm(ks[3], (DEPTH, 2, D, 2 * D_FF), D),
        "w_ffn_down": nrm(ks[4], (DEPTH, 2, D_FF, D), D_FF),
        "w_ple_gate": nrm(ks[5], (DEPTH, D, D), D),
        "w_ple_proj": nrm(ks[6], (DEPTH, PLE_DIM, D), PLE_DIM),
        "hg_lower_bounds": 0.5 * jax.random.normal(ks[7], (DEPTH, D), f32),
        "hg_w_in": nrm(ks[8], (N_A, D, 4 * D), D),
        "hg_g_norm": 1.0 + 0.02 * jax.random.normal(ks[9], (N_A, HG_DV), f32),
        "hg_w_out": nrm(ks[10], (N_A, D, D), D),
        "ml_w_qkvo": nrm(ks[13], (N_B, D, 2 * ML_HEADS * ML_DQK + 2 * D), D),
        "ml_w_if": nrm(ks[14], (N_B, D, 2 * ML_HEADS), D),
        "ml_b_if": b_if,
        "ml_norm": 1.0 + 0.02 * jax.random.normal(ks[15], (N_B, D), f32),
        "ml_w_out": nrm(ks[16], (N_B, D, D), D),
        "sw_w_qkv": nrm(ks[17], (N_C, D, (SW_HEADS + 2 * SW_KV_HEADS) * SW_HD), D),
        "sw_q_norm": 1.0 + 0.02 * jax.random.normal(ks[18], (N_C, SW_HD), f32),
        "sw_k_norm": 1.0 + 0.02 * jax.random.normal(ks[19], (N_C, SW_HD), f32),
        "sw_sinks": 0.5 * jax.random.normal(ks[20], (N_C, SW_HEADS), f32),
        "sw_w_o": nrm(ks[21], (N_C, SW_HEADS * SW_HD, D), SW_HEADS * SW_HD),
    }


def reference(x, p, norm_gains, w_ffn_gu, w_ffn_down, w_ple_gate, w_ple_proj,
              hg_lower_bounds, hg_w_in, hg_g_norm, hg_w_out,
              ml_w_qkvo, ml_w_if, ml_b_if, ml_norm, ml_w_out,
              sw_w_qkv, sw_q_norm, sw_k_norm, sw_sinks, sw_w_o):
    lbs = jnp.cumsum(jax.nn.softmax(hg_lower_bounds.astype(jnp.float32), axis=0), axis=0)
    lbs = lbs - lbs[0:1]
    for layer in range(DEPTH):
        g = norm_gains[layer]
        h = x + 0.5 * swiglu(rmsnorm(x, g[0]), w_ffn_gu[layer, 0], w_ffn_down[layer, 0])
        xn = rmsnorm(h, g[1])
        kind = layer % N_MIXERS
        j = layer // N_MIXERS
        if kind == 0:
            mix = hgrn2_mixer(xn, lbs[layer], hg_w_in[j], hg_g_norm[j], hg_w_out[j])
        elif kind == 1:
            mix = mlstm_mixer(xn, ml_w_qkvo[j], ml_w_if[j], ml_b_if[j], ml_norm[j], ml_w_out[j])
        else:
            mix = swa_mixer(xn, sw_w_qkv[j], sw_q_norm[j], sw_k_norm[j], sw_sinks[j], sw_w_o[j])
        h = h + mix
        h = h + 0.5 * swiglu(rmsnorm(h, g[2]), w_ffn_gu[layer, 1], w_ffn_down[layer, 1])
        gate = jax.nn.sigmoid(rmsnorm(h, g[3]) @ w_ple_gate[layer])
        x = h + gate * (p[layer] @ w_ple_proj[layer])
    return x
```

```python
from contextlib import ExitStack
import concourse.bass as bass
import concourse.mybir as mybir

F32 = mybir.dt.float32
BF16 = mybir.dt.bfloat16
AF = mybir.ActivationFunctionType
ALU = mybir.AluOpType
AX = mybir.AxisListType

ENGS = ("pe", "act", "dve", "pool", "sp")
SAME_ENGINE_RAW = True


class Tk:
    __slots__ = ("name", "w", "r", "dsem", "excl")

    def __init__(self, name=""):
        self.name = name
        self.excl = False
        self.w = None
        self.r = {}
        self.dsem = None


class Op:
    __slots__ = ("eng", "fn", "waits", "ordinal", "dma_sem", "signaled")

    def __init__(self, eng, fn):
        self.eng = eng
        self.fn = fn
        self.waits = {}
        self.ordinal = 0
        self.dma_sem = None
        self.signaled = False


class Prog:
    def __init__(self, nc, same_engine_sync=False):
        self.nc = nc
        self.ops = {e: [] for e in ENGS}
        self.seen = {e: {} for e in ENGS}
        self.n_dma_sems = 0
        self.dma_count = {}
        self.same_engine_sync = same_engine_sync
        self.all_tks = []
        self.floor = {}

    def tk(self, name=""):
        t = Tk(name)
        t.r = dict(self.floor)
        self.all_tks.append(t)
        return t

    def tks(self, n, name=""):
        return [self.tk(f"{name}{i}") for i in range(n)]

    def _need(self, op, ev, force=False, raw=False):
        if ev is None:
            return
        key, val = ev
        if key == ("eng", op.eng) and not force and not self.same_engine_sync:
            if not (raw and op.eng != "pe" and SAME_ENGINE_RAW):
                return
        if self.seen[op.eng].get(key, 0) >= val:
            return
        if op.waits.get(key, 0) < val:
            op.waits[key] = val

    def _commit_waits(self, op):
        for key, val in op.waits.items():
            if self.seen[op.eng].get(key, 0) < val:
                self.seen[op.eng][key] = val

    def op(self, eng, fn, reads=(), writes=()):
        ex = [t for t in reads if t.excl]
        exw = ex
        if ex:
            reads = [t for t in reads if not t.excl]
            writes = list(writes) + [t for t in ex if t not in writes]
        o = Op(eng, fn)
        lst = self.ops[eng]
        o.ordinal = len(lst) + 1
        for t in reads:
            self._need(o, t.w, raw=True)
        for t in writes:
            self._need(o, t.w, raw=True)
            for k, v in t.r.items():
                self._need(o, (k, v))
        self._commit_waits(o)
        lst.append(o)
        ev = (("eng", eng), o.ordinal)
        for t in reads:
            if t.r.get(ev[0], 0) < ev[1]:
                t.r[ev[0]] = ev[1]
        for t in writes:
            t.w = ev
            t.r = {}
        return o

    def ins(self, eng, meth, kw, reads=(), writes=()):
        return self.op(eng, lambda e: getattr(e, meth)(**kw), reads=reads, writes=writes)

    def dins(self, eng, meth, kw, reads=(), writes=(), sem_tk=None):
        return self.dma(eng, lambda e: getattr(e, meth)(**kw), reads=reads, writes=writes, sem_tk=sem_tk)

    def dma(self, eng, fn, reads=(), writes=(), sem_tk=None):
        o = Op(eng, fn)
        lst = self.ops[eng]
        o.ordinal = len(lst) + 1
        if sem_tk is None:
            sem_tk = writes[0] if writes else reads[0]
        if sem_tk.dsem is None:
            sem_tk.dsem = self.n_dma_sems
            self.n_dma_sems += 1
            self.dma_count[sem_tk.dsem] = 0
        s = sem_tk.dsem
        key = ("dma", s)
        prev = self.dma_count[s]
        if prev:
            self._need(o, (key, 16 * prev), force=True)
        for t in reads:
            self._need(o, t.w, force=True)
        for t in writes:
            self._need(o, t.w, force=True)
            for k, v in t.r.items():
                self._need(o, (k, v), force=True)
        self._commit_waits(o)
        self.dma_count[s] = prev + 1
        o.dma_sem = s
        lst.append(o)
        ev = (key, 16 * (prev + 1))
        for t in reads:
            if t.r.get(key, 0) < ev[1]:
                t.r[key] = ev[1]
        for t in writes:
            t.w = ev
            t.r = {}
        return o

    def _last_compute(self, e):
        for o in reversed(self.ops[e]):
            if o.dma_sem is None and o.fn is not None:
                return o.ordinal
        return 0

    def barrier(self):
        evs = {}
        for e in ENGS:
            lo = self._last_compute(e)
            if lo:
                evs[("eng", e)] = lo
        for s, c in self.dma_count.items():
            if c:
                evs[("dma", s)] = 16 * c
        self.floor = evs
        for t in self.all_tks:
            for k, v in evs.items():
                if t.r.get(k, 0) < v:
                    t.r[k] = v

    def final_wait(self, eng="sp"):
        o = Op(eng, None)
        o.ordinal = len(self.ops[eng]) + 1
        for e in ENGS:
            lo = self._last_compute(e)
            if lo and e != eng:
                self._need(o, (("eng", e), lo))
        for s, c in self.dma_count.items():
            if c:
                self._need(o, (("dma", s), 16 * c), force=True)
        self._commit_waits(o)
        self.ops[eng].append(o)

    def emit(self):
        nc = self.nc
        waited = {e: set() for e in ENGS}
        for e in ENGS:
            for o in self.ops[e]:
                for (kind, k), v in o.waits.items():
                    if kind == "eng":
                        waited[k].add(v)
        val_of = {}
        for e in ENGS:
            cnt = 0
            m = {}
            for o in self.ops[e]:
                if o.ordinal in waited[e]:
                    assert o.dma_sem is None and o.fn is not None, "engine-sem wait on a DMA/no-op"
                    cnt += 1
                    o.signaled = True
                    m[o.ordinal] = cnt
            val_of[e] = m
        with ExitStack() as es:
            esem = {e: es.enter_context(nc.semaphore(f"s_{e}")) for e in ENGS}
            dsem = [es.enter_context(nc.semaphore(f"d_{i}")) for i in range(self.n_dma_sems)]
            engobj = {"pe": "tensor", "act": "scalar", "dve": "vector", "pool": "gpsimd", "sp": "sync"}

            def run(ename):
                def body(eng):
                    for o in self.ops[ename]:
                        for (kind, k), v in o.waits.items():
                            if kind == "eng":
                                eng.wait_ge(esem[k], val_of[k][v])
                            else:
                                eng.wait_ge(dsem[k], v)
                        if o.fn is None:
                            continue
                        ins = o.fn(eng)
                        if o.dma_sem is not None:
                            ins.then_inc(dsem[o.dma_sem], 16)
                        elif o.signaled:
                            ins.then_inc(esem[ename], 1)
                return body

            with nc.Block() as block:
                for ename in ENGS:
                    if self.ops[ename]:
                        getattr(block, engobj[ename])(run(ename))
        stats = {e: len(self.ops[e]) for e in ENGS}
        stats["dma_sems"] = self.n_dma_sems
        return stats


import numpy as np
from concourse.bass_utils import run_bass_kernel_spmd

NCORES = 8
SEQ = 16384
NT = SEQ // NCORES
D = 1024
KC = 8
NTB = 4
TB = 512
D_FF = 2816
NFC = 22
PLE = 256
EPS = 1e-6
DEPTH = 4
FGROUPS = [(0, 8), (8, 8), (16, 6)]

C_GAIN = 0
C_LB = 128
C_HGN = 160
C_MLN = 162
C_BIF = 170
C_QG = 178
C_KG = 179
C_SINK = 180
C_SEL = 196
C_FLAG = 204
C_WIF = 205
NCST = 272
NCMAT = 512


class KB:
    def __init__(self, nc, P, es):
        self.nc = nc
        self.P = P
        NW = 50944
        self.arena = es.enter_context(nc.sbuf_tensor("arena", [128, NW], F32))
        self.ps = es.enter_context(nc.psum_tensor("ps", [128, 8 * 512], F32))
        self.off = 0
        a = self.arena
        self.xT = self.alloc(KC * NT).rearrange("p (k t) -> p k t", k=KC)
        self.xn = self.alloc(KC * NT // 2).bitcast(BF16).rearrange("p (k t) -> p k t", k=KC)
        self.R = self.alloc(KC * NT // 2).bitcast(BF16).rearrange("p (k t) -> p k t", k=KC)
        self.t_xT = [[P.tk(f"xT{k}_{t}") for t in range(NTB)] for k in range(KC)]
        self.t_xn = [[P.tk(f"xn{k}_{t}") for t in range(NTB)] for k in range(KC)]
        self.t_R = [[P.tk(f"R{k}_{t}") for t in range(NTB)] for k in range(KC)]
        self.NST = 3
        self.wst = [self.alloc(1024) for _ in range(self.NST)]
        self.t_wst = P.tks(self.NST, "wst")
        self.NWB = 8
        self.wbf = [self.alloc(512).bitcast(BF16) for _ in range(self.NWB)]
        self.t_wbf = P.tks(self.NWB, "wbf")
        self.i_wst = 0
        self.i_wbf = 0
        self.cst = self.alloc(NCST)
        self.t_cst = P.tk("cst")
        self.cmat = self.alloc(NCMAT // 2).bitcast(BF16)
        self.t_cmat = P.tk("cmat")
        self.ident = self.cmat[:, 0:128]
        self.ones = self.cmat[:, 128:256]
        self.bones = self.cmat[:, 256:384]
        self.gmask = self.cmat[:, 384:512]
        self.onesf = self.alloc(512)
        self.Moff = self.off
        self.NW = NW
        self.t_bank = P.tks(8, "bank")
        for t in self.t_bank:
            t.excl = True
        self.i_short = 0
        self.i_long = 0
        self.tmp_i = {}

    def alloc(self, nwords):
        ap = self.arena[:, self.off:self.off + nwords]
        self.off += nwords
        assert self.off <= self.NW if hasattr(self, "NW") else True
        return ap

    def bank(self, i):
        return self.ps[:, i * 512:(i + 1) * 512]

    def sbank(self):
        i = self.i_short
        self.i_short = (i + 1) % 5
        return self.bank(i), self.t_bank[i]

    def lbank(self):
        i = 5 + self.i_long
        self.i_long = (self.i_long + 1) % 3
        return self.bank(i), self.t_bank[i]

    def mreset(self):
        self.off = self.Moff
        self.tmps = {}

    def mtemp(self, name, nwords, nslots=2, dtype=F32):
        if name not in self.tmps:
            slots = []
            for s in range(nslots):
                ap = self.alloc(nwords)
                if dtype == BF16:
                    ap = ap.bitcast(BF16)
                slots.append((ap, self.P.tk(f"{name}{s}")))
            self.tmps[name] = [slots, 0]
        ent = self.tmps[name]
        ap, tk = ent[0][ent[1]]
        ent[1] = (ent[1] + 1) % len(ent[0])
        return ap, tk

    def load_consts(self, cst_d, cmat_d):
        P = self.P
        P.dins("sp", "dma_start", dict(out=self.cst, in_=cst_d), writes=[self.t_cst])
        st, t_st = self.wst[0], self.t_wst[0]
        P.dins("sp", "dma_start", dict(out=st[:, 0:NCMAT], in_=cmat_d), writes=[t_st])
        P.ins("dve", "tensor_copy", dict(out=self.cmat, in_=st[:, 0:NCMAT]), reads=[t_st], writes=[self.t_cmat])
        P.ins("dve", "memset", dict(ap=self.onesf, constant=1.0), writes=[self.t_cmat])

    def load_xT(self, src):
        P = self.P
        v = src.rearrange("(k p) t -> p k t", p=128)
        for k in range(KC):
            P.dins("sp", "dma_start", dict(out=self.xT[:, k, :], in_=v[:, k, :]), writes=self.t_xT[k])

    def store_xT(self, dst):
        P = self.P
        v = dst.rearrange("(k p) t -> p k t", p=128)
        for k in range(KC):
            P.dins("sp", "dma_start", dict(out=v[:, k, :], in_=self.xT[:, k, :]), reads=self.t_xT[k], sem_tk=self.t_xT[k][0])

    def load_w(self, src, nk, ncols=128, dup=False):
        P = self.P
        si = self.i_wst
        self.i_wst = (si + 1) % self.NST
        bi = self.i_wbf
        self.i_wbf = (bi + 1) % self.NWB
        st, t_st = self.wst[si], self.t_wst[si]
        wb, t_wb = self.wbf[bi], self.t_wbf[bi]
        srcv = src.rearrange("(k p) f -> p k f", p=128)
        if not dup:
            stv = st[:, 0:nk * ncols].rearrange("p (k f) -> p k f", k=nk)
            wbv = wb[:, 0:nk * ncols].rearrange("p (k f) -> p k f", k=nk)
            P.dins("sp", "dma_start", dict(out=stv, in_=srcv), writes=[t_st])
            P.ins("pool", "tensor_copy", dict(out=wb[:, 0:nk * ncols], in_=st[:, 0:nk * ncols]), reads=[t_st], writes=[t_wb])
            return wbv, t_wb
        stv = st[:, 0:nk * 128].rearrange("p (k f) -> p k f", k=nk)
        wbv = wb[:, 0:nk * 128].rearrange("p (k f) -> p k f", k=nk)
        P.dins("sp", "dma_start", dict(out=stv[:, :, 0:64], in_=srcv), writes=[t_st])
        P.dins("sp", "dma_start", dict(out=stv[:, :, 64:128], in_=srcv), writes=[t_st])
        P.ins("pool", "tensor_copy", dict(out=wb[:, 0:nk * 128], in_=st[:, 0:nk * 128]), reads=[t_st], writes=[t_wb])
        return wbv, t_wb

    def mm(self, out, t_out, lhsT, rhs, reads, start, stop):
        self.P.ins("pe", "matmul", dict(out=out, lhsT=lhsT, rhs=rhs, start=start, stop=stop), reads=reads, writes=[t_out])

    def rstd_from(self, pss, t_pss, n, cols=TB):
        P = self.P
        if cols == TB:
            ln, t_ln = self.mtemp("ln", TB)
            rs, t_rs = self.mtemp("rstd", TB)
        else:
            ln, t_ln = self.mtemp("lnG", cols, 1)
            rs, t_rs = self.mtemp("rstdG", cols, 1)
        P.ins("act", "activation", dict(out=ln[:, :cols], in_=pss[:, :cols], func=AF.Ln, scale=1.0 / n, bias=self.epsb), reads=[t_pss, self.t_cst, self.t_small], writes=[t_ln])
        P.ins("act", "activation", dict(out=rs[:, :cols], in_=ln[:, :cols], func=AF.Exp, scale=-0.5), reads=[t_ln], writes=[t_rs])
        return rs, t_rs

    def setup_eps(self):
        ap, tk = self.mtemp("epsb", 2, nslots=1)
        self.epsb = ap[:, 0:1]
        self.P.ins("dve", "memset", dict(ap=ap, constant=EPS), writes=[tk, self.t_cst])

    def norm(self, gcol):
        P = self.P
        for tb in range(NTB):
            ts = slice(tb * TB, (tb + 1) * TB)
            pss, t_pss = self.sbank()
            for k in range(KC):
                sq, t_sq = self.mtemp("sq", TB // 2, dtype=BF16)
                P.ins("act", "activation", dict(out=sq, in_=self.xT[:, k, ts], func=AF.Square), reads=[self.t_xT[k][tb]], writes=[t_sq])
                self.mm(pss, t_pss, self.ones, sq, [t_sq, self.t_cmat], k == 0, k == KC - 1)
            rs, t_rs = self.rstd_from(pss, t_pss, D)
            for k in range(KC):
                P.ins("dve", "scalar_tensor_tensor", dict(
                    out=self.xn[:, k, ts], in0=self.xT[:, k, ts], scalar=self.cst[:, gcol + k:gcol + k + 1],
                    in1=rs, op0=ALU.mult, op1=ALU.mult), reads=[self.t_xT[k][tb], t_rs, self.t_cst], writes=[self.t_xn[k][tb]])

    def ffn(self, wgu, wdn):
        P = self.P
        for (f0, nf) in FGROUPS:
            for fc in range(nf):
                f = f0 + fc
                wg, t_wg = self.load_w(wgu[:, f * 128:(f + 1) * 128], KC)
                wu, t_wu = self.load_w(wgu[:, D_FF + f * 128:D_FF + (f + 1) * 128], KC)
                for tb in range(NTB):
                    ts = slice(tb * TB, (tb + 1) * TB)
                    pg, t_pg = self.sbank()
                    pu, t_pu = self.sbank()
                    for k in range(KC):
                        self.mm(pg, t_pg, wg[:, k, :], self.xn[:, k, ts], [t_wg, self.t_xn[k][tb]], k == 0, k == KC - 1)
                    for k in range(KC):
                        self.mm(pu, t_pu, wu[:, k, :], self.xn[:, k, ts], [t_wu, self.t_xn[k][tb]], k == 0, k == KC - 1)
                    sg, t_sg = self.mtemp("sg", TB)
                    P.ins("act", "activation", dict(out=sg, in_=pg, func=AF.Silu), reads=[t_pg], writes=[t_sg])
                    P.ins("dve", "tensor_tensor", dict(out=self.R[:, fc, ts], in0=sg, in1=pu, op=ALU.mult), reads=[t_sg, t_pu], writes=[self.t_R[fc][tb]])
            for dc in range(KC):
                wd, t_wd = self.load_w(wdn[f0 * 128:(f0 + nf) * 128, dc * 128:(dc + 1) * 128], nf)
                for tb in range(NTB):
                    ts = slice(tb * TB, (tb + 1) * TB)
                    py, t_py = self.sbank()
                    for fc in range(nf):
                        self.mm(py, t_py, wd[:, fc, :], self.R[:, fc, ts], [t_wd, self.t_R[fc][tb]], fc == 0, fc == nf - 1)
                    P.ins("dve", "scalar_tensor_tensor", dict(
                        out=self.xT[:, dc, ts], in0=py, scalar=0.5, in1=self.xT[:, dc, ts], op0=ALU.mult, op1=ALU.add), reads=[t_py, self.t_xT[dc][tb]], writes=[self.t_xT[dc][tb]])

    def wout(self, W):
        P = self.P
        for dc in range(KC):
            w, t_w = self.load_w(W[:, dc * 128:(dc + 1) * 128], KC)
            for tb in range(NTB):
                ts = slice(tb * TB, (tb + 1) * TB)
                py, t_py = self.sbank()
                for k in range(KC):
                    self.mm(py, t_py, w[:, k, :], self.R[:, k, ts], [t_w, self.t_R[k][tb]], k == 0, k == KC - 1)
                P.ins("dve", "tensor_tensor", dict(out=self.xT[:, dc, ts], in0=py, in1=self.xT[:, dc, ts], op=ALU.add), reads=[t_py, self.t_xT[dc][tb]], writes=[self.t_xT[dc][tb]])

    def ple(self, Wg, Wp, pT):
        P = self.P
        pb, t_pb0 = self.mtemp("pTb", 2 * NT // 2, nslots=1, dtype=BF16)
        pbv = pb.rearrange("p (c t) -> p c t", c=2)
        t_pb = [[P.tk(f"pb{c}_{t}") for t in range(NTB)] for c in range(2)]
        pTv = pT.rearrange("(c p) t -> p c t", p=128)
        for c in range(2):
            for half in range(2):
                si = self.i_wst
                self.i_wst = (si + 1) % self.NST
                st, t_st = self.wst[si], self.t_wst[si]
                P.dins("sp", "dma_start", dict(out=st[:, 0:1024], in_=pTv[:, c, half * 1024:(half + 1) * 1024]), writes=[t_st])
                P.ins("pool", "tensor_copy", dict(out=pbv[:, c, half * 1024:(half + 1) * 1024], in_=st[:, 0:1024]), reads=[t_st], writes=[t_pb[c][2 * half], t_pb[c][2 * half + 1]])
        for dc in range(KC):
            wg, t_wg = self.load_w(Wg[:, dc * 128:(dc + 1) * 128], KC)
            wp, t_wp = self.load_w(Wp[:, dc * 128:(dc + 1) * 128], 2)
            for tb in range(NTB):
                ts = slice(tb * TB, (tb + 1) * TB)
                pg, t_pg = self.sbank()
                pp, t_pp = self.sbank()
                for k in range(KC):
                    self.mm(pg, t_pg, wg[:, k, :], self.xn[:, k, ts], [t_wg, self.t_xn[k][tb]], k == 0, k == KC - 1)
                for c in range(2):
                    self.mm(pp, t_pp, wp[:, c, :], pbv[:, c, ts], [t_wp, t_pb[c][tb]], c == 0, c == 1)
                sg, t_sg = self.mtemp("sg", TB)
                P.ins("act", "activation", dict(out=sg, in_=pg, func=AF.Sigmoid), reads=[t_pg], writes=[t_sg])
                t2, t_t2 = self.mtemp("t2", TB)
                P.ins("dve", "tensor_tensor", dict(out=t2, in0=sg, in1=pp, op=ALU.mult), reads=[t_sg, t_pp], writes=[t_t2])
                P.ins("pool", "tensor_tensor", dict(out=self.xT[:, dc, ts], in0=t2, in1=self.xT[:, dc, ts], op=ALU.add), reads=[t_t2, self.t_xT[dc][tb]], writes=[self.t_xT[dc][tb]])


def make_cst(inp, core):
    c = np.zeros((128, NCST), np.float32)
    g = np.asarray(inp["norm_gains"], np.float32)
    c[:, C_GAIN:C_GAIN + 128] = g.reshape(16, 8, 128).transpose(2, 0, 1).reshape(128, 128)
    lb = np.asarray(inp["hg_lower_bounds"], np.float32)
    c[:, C_LB:C_LB + 32] = lb.reshape(4, 8, 128).transpose(2, 0, 1).reshape(128, 32)
    c[:, C_HGN:C_HGN + 2] = np.asarray(inp["hg_g_norm"], np.float32).T
    c[:, C_MLN:C_MLN + 8] = np.asarray(inp["ml_norm"], np.float32).reshape(8, 128).T
    c[:, C_BIF:C_BIF + 8] = np.asarray(inp["ml_b_if"], np.float32).reshape(1, 8)
    c[:, C_QG] = np.tile(np.asarray(inp["sw_q_norm"], np.float32).reshape(64), 2)
    c[:, C_KG] = np.tile(np.asarray(inp["sw_k_norm"], np.float32).reshape(64), 2)
    c[:, C_SINK:C_SINK + 16] = np.asarray(inp["sw_sinks"], np.float32).reshape(1, 16)
    c[:, C_SEL + core] = 1.0
    c[:, C_FLAG] = 0.0 if core == 0 else 1.0
    wif = np.asarray(inp["ml_w_if"], np.float32).reshape(8, 128, 8)
    c[:, C_WIF:C_WIF + 64] = wif.transpose(1, 0, 2).reshape(128, 64)
    return c


def make_cmat():
    m = np.zeros((128, NCMAT), np.float32)
    m[:, 0:128] = np.eye(128, dtype=np.float32)
    m[:, 128:256] = 1.0
    i = np.arange(128)
    m[:, 256:384] = (i[:, None] // 64 == i[None, :] // 64).astype(np.float32)
    m[:, 384:512] = ((i[:, None] // 64 == i[None, :] // 64) & (i[:, None] <= i[None, :])).astype(np.float32)
    return m


GB = 256
NGB = NT // GB
DBG = {}


def _persist_small(kb):
    kb.lbv = kb.alloc(32)
    kb.oml = kb.alloc(32)
    kb.b15 = kb.alloc(8)
    kb.cb = kb.alloc(2)
    kb.t_small = kb.P.tk("small")
    kb.Moff = kb.off


def small_setup(kb):
    P = kb.P
    cst = kb.cst
    ex, t_ex = kb.mtemp("lbe", 32, nslots=1)
    sm, t_sm = kb.mtemp("lbs", 8, nslots=1)
    cu, t_cu = kb.mtemp("lbc", 8, nslots=1)
    P.ins("dve", "memset", dict(ap=kb.cb[:, 0:1], constant=EPS), writes=[kb.t_small])
    P.ins("dve", "memset", dict(ap=kb.cb[:, 1:2], constant=1.0), writes=[kb.t_small])
    kb.epsb = kb.cb[:, 0:1]
    kb.oneb = kb.cb[:, 1:2]
    P.ins("dve", "tensor_scalar", dict(out=kb.b15, in0=cst[:, C_BIF:C_BIF + 8], scalar1=1.0 / 15.0, scalar2=None, op0=ALU.mult), reads=[kb.t_cst], writes=[kb.t_small])
    P.ins("act", "activation", dict(out=ex, in_=cst[:, C_LB:C_LB + 32], func=AF.Exp), reads=[kb.t_cst], writes=[t_ex])
    P.ins("dve", "tensor_tensor", dict(out=sm, in0=ex[:, 0:8], in1=ex[:, 8:16], op=ALU.add), reads=[t_ex], writes=[t_sm])
    P.ins("dve", "tensor_tensor", dict(out=sm, in0=sm, in1=ex[:, 16:24], op=ALU.add), reads=[t_ex], writes=[t_sm])
    P.ins("dve", "tensor_tensor", dict(out=sm, in0=sm, in1=ex[:, 24:32], op=ALU.add), reads=[t_ex], writes=[t_sm])
    P.ins("dve", "reciprocal", dict(out=sm, in_=sm), writes=[t_sm])
    P.ins("dve", "memset", dict(ap=kb.lbv[:, 0:8], constant=0.0), writes=[kb.t_small])
    P.ins("dve", "tensor_copy", dict(out=cu, in_=ex[:, 8:16]), reads=[t_ex], writes=[t_cu])
    for l in (1, 2, 3):
        if l > 1:
            P.ins("dve", "tensor_tensor", dict(out=cu, in0=cu, in1=ex[:, l * 8:l * 8 + 8], op=ALU.add), reads=[t_ex], writes=[t_cu])
        P.ins("dve", "tensor_tensor", dict(out=kb.lbv[:, l * 8:l * 8 + 8], in0=cu, in1=sm, op=ALU.mult), reads=[t_cu, t_sm], writes=[kb.t_small])
    P.ins("dve", "tensor_scalar", dict(out=kb.oml, in0=kb.lbv, scalar1=-1.0, scalar2=1.0, op0=ALU.mult, op1=ALU.add), writes=[kb.t_small])


def gla(kb, kind, stage, l, Win, gS=None, gD=None, sendS=None, sendD=None):
    P = kb.P
    cst = kb.cst
    xn, t_xn = kb.xn, kb.t_xn
    if kind == "hg":
        H, DV, DVr, sgn = 8, 128, 128, 1.0
    else:
        H, DV, DVr, sgn = 4, 384, 256, -1.0
    ndv = DV // 128
    nvr = DVr // 128
    B = (stage == "B")
    jn = l // 3
    dtile, t_dtile = kb.mtemp("dtile", 8, nslots=1)
    vts = []
    for s in range(2):
        ap, tk = kb.mtemp("vtm", 2 * DV // 2, 2, BF16)
        v = ap.rearrange("p (t d) -> p t d", t=2)
        if kind == "ml":
            P.ins("dve", "memset", dict(ap=v[:, :, 256:384], constant=1.0), writes=[tk])
        vts.append((v, tk))
    for h in range(DBG.get("H", H)):
        if kind == "hg":
            wq = kb.load_w(Win[:, h * 128:(h + 1) * 128], KC) if B else None
            wf = kb.load_w(Win[:, 1024 + h * 128:1024 + (h + 1) * 128], KC)
            wv = [kb.load_w(Win[:, 2048 + h * 128:2048 + (h + 1) * 128], KC)]
            wog = [kb.load_w(Win[:, 3072 + h * 128:3072 + (h + 1) * 128], KC)] if B else None
        else:
            wq = kb.load_w(Win[:, h * 128:(h + 1) * 128], KC) if B else None
            wf = kb.load_w(Win[:, 512 + h * 128:512 + (h + 1) * 128], KC)
            wv = [kb.load_w(Win[:, 1024 + h * 256 + c * 128:1024 + h * 256 + (c + 1) * 128], KC) for c in range(2)]
            wog = [kb.load_w(Win[:, 2048 + h * 256 + c * 128:2048 + h * 256 + (c + 1) * 128], KC) for c in range(2)] if B else None
            wr, t_wr = kb.mtemp("wifrep", 1024, 1, BF16)
            wrv = wr.rearrange("p (k g f) -> p k g f", k=KC, g=2)
            for gi, col in enumerate((h, 4 + h)):
                for k in range(KC):
                    cc_ = C_WIF + k * 8 + col
                    P.ins("pool", "tensor_copy", dict(out=wrv[:, k, gi, :], in_=cst[:, cc_:cc_ + 1].to_broadcast([128, 128])), reads=[kb.t_cst], writes=[t_wr])
        S, t_S = kb.mtemp("S", DV, 1)
        P.ins("dve", "memset", dict(ap=S, constant=0.0), writes=[t_S])
        if B:
            Rr, t_Rr = kb.mtemp("Rr", DV, 1)
            Dg, t_Dg = kb.mtemp("Dg", 8, 1)
            P.ins("dve", "memset", dict(ap=Rr, constant=0.0), writes=[t_Rr])
            P.dins("sp", "dma_start", dict(out=Dg, in_=gD[h]), writes=[t_Dg])
            for jj in range(NCORES - 1):
                Sj, t_Sj = kb.mtemp("Sj", DV, 1)
                P.dins("sp", "dma_start", dict(out=Sj, in_=gS[h, jj]), writes=[t_Sj])
                P.ins("dve", "scalar_tensor_tensor", dict(out=Rr, in0=Rr, scalar=Dg[:, jj:jj + 1], in1=Sj, op0=ALU.mult, op1=ALU.add), reads=[t_Dg, t_Sj], writes=[t_Rr])
                P.ins("dve", "scalar_tensor_tensor", dict(out=S, in0=Rr, scalar=cst[:, C_SEL + jj + 1:C_SEL + jj + 2], in1=S, op0=ALU.mult, op1=ALU.add), reads=[t_Rr, kb.t_cst], writes=[t_S])
        if B:
            P.barrier()
        if B and sendS is not None and not DBG.get("dumpSend"):
            P.dins("sp", "dma_start", dict(out=sendS[h], in_=S), reads=[t_S])
        prevBb = None
        for gb in range(DBG.get("NGB", NGB)):
            tb = gb // 2
            gs = slice(gb * GB, (gb + 1) * GB)
            rx = lambda k: [t_xn[k][tb]]
            if B:
                pq, t_pq = kb.sbank()
                for k in range(KC):
                    kb.mm(pq[:, :GB], t_pq, wq[0][:, k, :], xn[:, k, gs], [wq[1]] + rx(k), k == 0, k == KC - 1)
                tq, t_tq = kb.mtemp("tq", GB, 2)
                fq = AF.Silu if kind == "hg" else AF.Copy
                P.ins("act", "activation", dict(out=tq, in_=pq[:, :GB], func=fq), reads=[t_pq], writes=[t_tq])
            pf, t_pf = kb.sbank()
            for k in range(KC):
                kb.mm(pf[:, :GB], t_pf, wf[0][:, k, :], xn[:, k, gs], [wf[1]] + rx(k), k == 0, k == KC - 1)
            kf, t_kf = kb.mtemp("tb", GB, 2)
            tc, t_tc = kb.mtemp("tc", GB, 1)
            if kind == "hg":
                hc = l * 8 + h
                P.ins("act", "activation", dict(out=kf, in_=pf[:, :GB], func=AF.Sigmoid), reads=[t_pf], writes=[t_kf])
                P.ins("dve", "tensor_scalar", dict(out=kf, in0=kf, scalar1=kb.oml[:, hc:hc + 1], scalar2=kb.lbv[:, hc:hc + 1],
                                                                    op0=ALU.mult, op1=ALU.add), reads=[kb.t_small], writes=[t_kf])
                P.ins("act", "activation", dict(out=tc, in_=kf, func=AF.Ln), reads=[t_kf], writes=[t_tc])
                P.ins("pool", "tensor_scalar", dict(out=kf, in0=kf, scalar1=-1.0, scalar2=1.0, op0=ALU.mult, op1=ALU.add), reads=[t_tc], writes=[t_kf])
            else:
                P.ins("act", "activation", dict(out=kf, in_=pf[:, :GB], func=AF.Copy, scale=128.0 ** -0.5), reads=[t_pf], writes=[t_kf])
                pgi, t_pgi = kb.sbank()
                for k in range(KC):
                    kb.mm(pgi[:, :GB], t_pgi, wrv[:, k, 0, :], xn[:, k, gs], [t_wr] + rx(k), k == 0, k == KC - 1)
                pgf, t_pgf = kb.sbank()
                for k in range(KC):
                    kb.mm(pgf[:, :GB], t_pgf, wrv[:, k, 1, :], xn[:, k, gs], [t_wr] + rx(k), k == 0, k == KC - 1)
                ti, t_ti = kb.mtemp("ti", GB, 1)
                P.ins("act", "activation", dict(out=ti, in_=pgi[:, :GB], func=AF.Tanh, scale=1.0 / 15.0, bias=kb.b15[:, h:h + 1]), reads=[t_pgi, kb.t_small], writes=[t_ti])
                P.ins("act", "activation", dict(out=tc, in_=pgf[:, :GB], func=AF.Tanh, scale=1.0 / 15.0, bias=kb.b15[:, 4 + h:5 + h]), reads=[t_pgf, kb.t_small], writes=[t_tc])
                P.ins("act", "activation", dict(out=tc, in_=tc, func=AF.Exp, scale=-15.0), writes=[t_tc])
                P.ins("act", "activation", dict(out=tc, in_=tc, func=AF.Ln, bias=kb.oneb), reads=[kb.t_small], writes=[t_tc])
            Bb, t_Bb = kb.mtemp("Bb", GB + 4, 2)
            if gb == 0:
                P.ins("dve", "memset", dict(ap=Bb[:, 0:1], constant=0.0), writes=[t_Bb])
            else:
                P.ins("dve", "tensor_copy", dict(out=Bb[:, 0:1], in_=prevBb[0][:, GB:GB + 1]), reads=[prevBb[1]], writes=[t_Bb])
            P.ins("dve", "tensor_tensor_scan", dict(out=Bb[:, 1:GB + 1], data0=kb.onesf[:, :GB], data1=tc, initial=Bb[:, 0:1],
                                                                    op0=ALU.mult, op1=ALU.add), reads=[t_tc, kb.t_cmat], writes=[t_Bb])
            prevBb = (Bb, t_Bb)
            nch = GB // 64
            Bs = Bb[:, 0:GB - 63:64]
            Bm = Bb[:, 32:GB - 31:64]
            Be = Bb[:, 64:GB + 1:64]
            te, t_te = kb.mtemp("te", GB, 1)
            P.ins("dve", "tensor_tensor", dict(
                out=te.rearrange("p (c l) -> p c l", l=64), in0=Bb[:, 1:GB + 1].rearrange("p (c l) -> p c l", l=64),
                in1=Bm.unsqueeze(2).to_broadcast([128, nch, 64]), op=ALU.subtract), reads=[t_Bb], writes=[t_te])
            tf, t_tf = kb.mtemp("tf", GB, 1)
            if B:
                P.ins("act", "activation", dict(out=tf, in_=te, func=AF.Exp, scale=sgn), reads=[t_te], writes=[t_tf])
            if kind == "ml":
                P.ins("dve", "scalar_tensor_tensor", dict(out=te, in0=ti, scalar=15.0, in1=te, op0=ALU.mult, op1=ALU.add), reads=[t_ti, t_tf], writes=[t_te])
                P.ins("act", "activation", dict(out=te, in_=te, func=AF.Exp), writes=[t_te])
            else:
                P.ins("act", "activation", dict(out=te, in_=te, func=AF.Exp, scale=-1.0), reads=[t_tf], writes=[t_te])
            qt, t_qt = kb.mtemp("qt", GB // 2, 2, BF16)
            kt, t_kt = kb.mtemp("kt", GB // 2, 2, BF16)
            if B:
                P.ins("dve", "tensor_tensor", dict(out=qt, in0=tq, in1=tf, op=ALU.mult), reads=[t_tq, t_tf], writes=[t_qt])
            P.ins("pool", "tensor_tensor", dict(out=kt, in0=kf, in1=te, op=ALU.mult), reads=[t_kf, t_te], writes=[t_kt])
            dd, t_dd = kb.mtemp("dd", 12, 2)
            cc, t_cc = kb.mtemp("cc", 12, 2)
            P.ins("dve", "tensor_tensor", dict(out=dd[:, 0:nch], in0=Bm, in1=Bs, op=ALU.subtract), reads=[t_Bb], writes=[t_dd])
            P.ins("dve", "tensor_tensor", dict(out=dd[:, 4:4 + nch], in0=Be, in1=Bm, op=ALU.subtract), reads=[t_Bb], writes=[t_dd])
            P.ins("dve", "tensor_tensor", dict(out=dd[:, 8:8 + nch], in0=Be, in1=Bs, op=ALU.subtract), reads=[t_Bb], writes=[t_dd])
            P.ins("act", "activation", dict(out=cc, in_=dd, func=AF.Exp, scale=sgn), reads=[t_dd], writes=[t_cc])
            vtm, t_vtm = vts[gb % 2]
            for tt in range(2):
                pv, t_pv = kb.sbank()
                t0 = gb * GB + tt * 128
                for c in range(nvr):
                    for k in range(KC):
                        kb.mm(pv[:, c * 128:(c + 1) * 128], t_pv, xn[:, k, t0:t0 + 128], wv[c][0][:, k, :], [wv[c][1]] + rx(k), k == 0, k == KC - 1)
                P.ins("act", "activation", dict(out=vtm[:, tt, 0:DVr], in_=pv[:, 0:DVr], func=AF.Copy), reads=[t_pv], writes=[t_vtm])
            ptr, t_ptr = kb.sbank()
            ptrb = ptr[:, 0:GB // 2].bitcast(BF16)
            for tt in range(2):
                P.ins("pe", "transpose", dict(out=ptrb[:, tt * 128:(tt + 1) * 128], in_=kt[:, tt * 128:(tt + 1) * 128], identity=kb.ident), reads=[t_kt, kb.t_cmat], writes=[t_ptr])
            ktm0, t_ktm = kb.mtemp("ktm", GB // 2, 2, BF16)
            ktm = ktm0.rearrange("p (t d) -> p t d", t=2)
            P.ins("dve", "tensor_copy", dict(out=ktm0, in_=ptrb), reads=[t_ptr], writes=[t_ktm])
            if B:
                po = [kb.lbank() for _ in range(ndv)]
            for pp in range(2):
                psl = slice(pp * 128, (pp + 1) * 128)
                if B:
                    pa, t_pa = kb.sbank()
                    kb.mm(pa[:, :128], t_pa, kt[:, psl], qt[:, psl], [t_kt, t_qt], True, True)
                    Am, t_Am = kb.mtemp("Am", 64, 2, BF16)
                    P.ins("dve", "tensor_tensor", dict(out=Am, in0=pa[:, :128], in1=kb.gmask, op=ALU.mult), reads=[t_pa, kb.t_cmat], writes=[t_Am])
                    for dvc in range(ndv):
                        kb.mm(po[dvc][0][:, psl], po[dvc][1], vtm[:, pp, dvc * 128:(dvc + 1) * 128], Am, [t_vtm, t_Am], True, False)
                for c2 in range(2):
                    c = pp * 2 + c2
                    cs = slice(c * 64, (c + 1) * 64)
                    if B:
                        S1, t_S1 = kb.mtemp("S1", DV // 2, 2, BF16)
                        P.ins("dve", "tensor_scalar", dict(out=S1, in0=S, scalar1=cc[:, c:c + 1], scalar2=None, op0=ALU.mult), reads=[t_S, t_cc], writes=[t_S1])
                        for dvc in range(ndv):
                            kb.mm(po[dvc][0][:, cs], po[dvc][1], S1[:, dvc * 128:(dvc + 1) * 128], qt[:, cs], [t_S1, t_qt], False, True)
                    pP, t_pP = kb.sbank()
                    kb.mm(pP[:, :DV], t_pP, ktm[c2 * 64:(c2 + 1) * 64, pp, :], vtm[c2 * 64:(c2 + 1) * 64, pp, :], [t_ktm, t_vtm], True, True)
                    tP, t_tP = kb.mtemp("tP", DV, 2)
                    P.ins("act", "activation", dict(out=tP, in_=pP[:, :DV], func=AF.Copy, scale=cc[:, 4 + c:5 + c]), reads=[t_pP, t_cc], writes=[t_tP])
                    P.ins("dve", "scalar_tensor_tensor", dict(out=S, in0=S, scalar=cc[:, 8 + c:9 + c], in1=tP, op0=ALU.mult, op1=ALU.add), reads=[t_tP, t_cc], writes=[t_S])
            if not B:
                continue
            pss, t_pss = kb.sbank()
            if kind == "hg":
                sq, t_sq = kb.mtemp("sqG", GB // 2, 2, BF16)
                P.ins("act", "activation", dict(out=sq[:, :GB], in_=po[0][0][:, :GB], func=AF.Square), reads=[po[0][1]], writes=[t_sq])
                kb.mm(pss[:, :GB], t_pss, kb.ones, sq[:, :GB], [t_sq, kb.t_cmat], True, True)
                rs, t_rs = kb.rstd_from(pss, t_pss, 128, cols=GB)
                pog, t_pog = kb.sbank()
                for k in range(KC):
                    kb.mm(pog[:, :GB], t_pog, wog[0][0][:, k, :], xn[:, k, gs], [wog[0][1]] + rx(k), k == 0, k == KC - 1)
                so, t_so = kb.mtemp("so", GB, 1)
                P.ins("act", "activation", dict(out=so, in_=pog[:, :GB], func=AF.Silu), reads=[t_pog], writes=[t_so])
                on, t_on = kb.mtemp("on", GB, 1)
                P.ins("dve", "scalar_tensor_tensor", dict(out=on, in0=po[0][0][:, :GB], scalar=cst[:, C_HGN + jn:C_HGN + jn + 1], in1=rs[:, :GB],
                                                                                 op0=ALU.mult, op1=ALU.mult), reads=[po[0][1], t_rs, kb.t_cst], writes=[t_on])
                P.ins("pool", "tensor_tensor", dict(out=kb.R[:, h, gs], in0=on, in1=so, op=ALU.mult), reads=[t_on, t_so], writes=[kb.t_R[h][tb]])
            else:
                dn, t_dn = kb.mtemp("dn", GB, 1)
                P.ins("act", "activation", dict(out=dn, in_=po[2][0][:, :GB], func=AF.Abs), reads=[po[2][1]], writes=[t_dn])
                P.ins("dve", "tensor_scalar", dict(out=dn, in0=dn, scalar1=1.0, scalar2=None, op0=ALU.max), writes=[t_dn])
                P.ins("dve", "reciprocal", dict(out=dn, in_=dn), writes=[t_dn])
                hhs = []
                for dvc in range(2):
                    hh, t_hh = kb.mtemp("hh", GB, 2)
                    hhs.append((hh, t_hh))
                    P.ins("dve", "tensor_tensor", dict(out=hh, in0=po[dvc][0][:, :GB], in1=dn, op=ALU.mult), reads=[po[dvc][1], t_dn], writes=[t_hh])
                    sq, t_sq = kb.mtemp("sqG", GB // 2, 2, BF16)
                    P.ins("act", "activation", dict(out=sq[:, :GB], in_=hh, func=AF.Square), reads=[t_hh], writes=[t_sq])
                    kb.mm(pss[:, :GB], t_pss, kb.ones, sq[:, :GB], [t_sq, kb.t_cmat], dvc == 0, dvc == 1)
                rs, t_rs = kb.rstd_from(pss, t_pss, 256, cols=GB)
                for dvc in range(2):
                    hh, t_hh = hhs[dvc]
                    pog, t_pog = kb.sbank()
                    for k in range(KC):
                        kb.mm(pog[:, :GB], t_pog, wog[dvc][0][:, k, :], xn[:, k, gs], [wog[dvc][1]] + rx(k), k == 0, k == KC - 1)
                    so, t_so = kb.mtemp("so", GB, 1)
                    P.ins("act", "activation", dict(out=so, in_=pog[:, :GB], func=AF.Sigmoid), reads=[t_pog], writes=[t_so])
                    mc = C_MLN + h * 2 + dvc
                    P.ins("dve", "scalar_tensor_tensor", dict(out=hh, in0=hh, scalar=cst[:, mc:mc + 1], in1=rs[:, :GB], op0=ALU.mult, op1=ALU.mult), reads=[t_rs, kb.t_cst], writes=[t_hh])
                    hc2 = h * 2 + dvc
                    P.ins("pool", "tensor_tensor", dict(out=kb.R[:, hc2, gs], in0=hh, in1=so, op=ALU.mult), reads=[t_hh, t_so], writes=[kb.t_R[hc2][tb]])
        if B and sendS is not None and DBG.get("dumpSend"):
            P.dins("sp", "dma_start", dict(out=sendS[h], in_=S), reads=[t_S])
        if not B:
            P.dins("sp", "dma_start", dict(out=sendS[h], in_=S), reads=[t_S])
            P.ins("act", "activation", dict(out=dtile[:, h:h + 1], in_=prevBb[0][:, GB:GB + 1], func=AF.Exp, scale=sgn), reads=[prevBb[1]], writes=[t_dtile])
    if not B and not DBG.get("nosendD"):
        P.dins("sp", "dma_start", dict(out=sendD, in_=dtile[:, 0:H]), reads=[t_dtile])


KINDS = ["hg", "ml", "sw", "hg"]
MIXW = {"hg": 4096, "ml": 3072, "sw": 1536}
GLA_H = {"hg": (8, 128), "ml": (4, 384)}


def build_launch(lB, lA):
    nc = bass.Bass("TRN2", target_bir_lowering=False)
    P = Prog(nc)

    def din(name, shape, dt=F32):
        return nc.dram_tensor(name, list(shape), dt, kind="ExternalInput").ap()

    def dout(name, shape, dt=F32):
        return nc.dram_tensor(name, list(shape), dt, kind="ExternalOutput").ap()

    xin = din("xin", [D, NT])
    cst = din("cst", [128, NCST])
    cmat = din("cmat", [128, NCMAT])
    dr = {}
    if lB is not None:
        kB = KINDS[lB]
        dr["wmixB"] = din("wmixB", [D, MIXW[kB]])
        dr["woB"] = din("woB", [D, D])
        dr["wguB"] = din("wguB", [D, 2 * D_FF])
        dr["wdnB"] = din("wdnB", [D_FF, D])
        dr["wplg"] = din("wplg", [D, D])
        dr["wplp"] = din("wplp", [PLE, D])
        dr["pT"] = din("pT", [PLE, NT])
        if kB in GLA_H:
            H, DV = GLA_H[kB]
            dr["gS"] = din("gS", [H, NCORES, 128, DV])
            dr["gD"] = din("gD", [H, 128, NCORES])
        else:
            dr["halo"] = din("halo", [D, 128])
            dr["mexp"] = din("mexp", [128, 16 * 2 * 128])
    if lA is not None:
        kA = KINDS[lA]
        dr["wguA"] = din("wguA", [D, 2 * D_FF])
        dr["wdnA"] = din("wdnA", [D_FF, D])
        if kA in GLA_H and not DBG.get("nogla"):
            H, DV = GLA_H[kA]
            dr["wmixA"] = din("wmixA", [D, MIXW[kA]])
            dr["sendS"] = dout("sendS", [H, 128, DV])
            dr["sendD"] = dout("sendD", [128, H])
    xout = dout("xout", [D, NT])
    with ExitStack() as es:
        kb = KB(nc, P, es)
        _persist_small(kb)
        kb.mreset()
        kb.load_consts(cst, cmat)
        if DBG.get("nosmall"):
            kb.t_small = kb.t_cst
            kb.setup_eps()
        else:
            small_setup(kb)
        kb.load_xT(xin)
        if lB is not None:
            g0 = C_GAIN + lB * 32
            P.barrier(); kb.mreset()
            kb.norm(g0 + 8)
            P.barrier(); kb.mreset()
            if kB in GLA_H:
                if DBG.get("dumpS"):
                    dr["dbgS"] = dout("dbgS", [GLA_H[kB][0], 128, GLA_H[kB][1]])
                gla(kb, kB, "B", lB, dr["wmixB"], gS=dr["gS"], gD=dr["gD"], sendS=dr.get("dbgS"))
            else:
                swa(kb, dr["wmixB"], dr["halo"], dr["mexp"], g0 + 8)
            P.barrier(); kb.mreset()
            if DBG.get("stopAfterMixer"):
                for k in range(KC):
                    for tb in range(NTB):
                        ts = slice(tb * TB, (tb + 1) * TB)
                        P.ins("dve", "tensor_copy", dict(out=kb.xT[:, k, ts], in_=kb.R[:, k, ts]), reads=[kb.t_R[k][tb]], writes=[kb.t_xT[k][tb]])
                kb.store_xT(xout)
                P.final_wait("sp")
                return nc, P.emit()
            kb.wout(dr["woB"])
            kb.norm(g0 + 16)
            kb.ffn(dr["wguB"], dr["wdnB"])
            kb.norm(g0 + 24)
            kb.ple(dr["wplg"], dr["wplp"], dr["pT"])
        if lA is not None:
            g0 = C_GAIN + lA * 32
            P.barrier(); kb.mreset()
            kb.norm(g0)
            if not DBG.get("noffn"):
                kb.ffn(dr["wguA"], dr["wdnA"])
            if kA in GLA_H and not DBG.get("nogla"):
                kb.norm(g0 + 8)
                P.barrier(); kb.mreset()
                gla(kb, kA, "A", lA, dr["wmixA"], sendS=dr["sendS"], sendD=dr["sendD"])
        kb.store_xT(xout)
        P.final_wait("sp")
        stats = P.emit()
    return nc, stats


def make_mexp():
    m = np.zeros((128, 16, 2, 128), np.float32)
    sidx = np.arange(128)[:, None].astype(np.float32)
    tidx = np.arange(128)[None, :].astype(np.float32)
    for h in range(16):
        slope = 2.0 ** (-8.0 * (h + 1) / 16)
        d_cur = tidx - sidx
        m[:, h, 1, :] = np.where(d_cur >= 0, np.exp(-slope * np.maximum(d_cur, 0)), 0.0)
        d_prev = tidx - sidx + 128.0
        m[:, h, 0, :] = np.where(d_prev < 128, np.exp(-slope * d_prev), 0.0)
    return m.reshape(128, 16 * 2 * 128)


def swa(kb, W, halo, mexp, gcol):
    P = kb.P
    cst = kb.cst
    xn, t_xn = kb.xn, kb.t_xn
    NB = NT // 128
    si = kb.i_wst
    kb.i_wst = (si + 1) % kb.NST
    hst, t_hst = kb.wst[si], kb.t_wst[si]
    hstv = hst.rearrange("p (k t) -> p k t", k=KC)
    P.dins("sp", "dma_start", dict(out=hstv, in_=halo.rearrange("(k p) t -> p k t", p=128)), writes=[t_hst])
    hxn0, t_hxn = kb.mtemp("hxn", KC * 128 // 2, 1, BF16)
    hxn = hxn0.rearrange("p (k t) -> p k t", k=KC)
    pss, t_pss = kb.sbank()
    for k in range(KC):
        sq, t_sq = kb.mtemp("sq", TB // 2, 2, BF16)
        P.ins("act", "activation", dict(out=sq[:, :128], in_=hstv[:, k, :], func=AF.Square), reads=[t_hst], writes=[t_sq])
        kb.mm(pss[:, :128], t_pss, kb.ones, sq[:, :128], [t_sq, kb.t_cmat], k == 0, k == KC - 1)
    rs, t_rs = kb.rstd_from(pss, t_pss, D, cols=128)
    for k in range(KC):
        P.ins("dve", "scalar_tensor_tensor", dict(out=hxn[:, k, :], in0=hstv[:, k, :], scalar=cst[:, gcol + k:gcol + k + 1], in1=rs[:, :128],
                                                  op0=ALU.mult, op1=ALU.mult), reads=[t_hst, t_rs, kb.t_cst], writes=[t_hxn])
    mxb, t_mxb = kb.mtemp("mexpb", 4096 // 2, 1, BF16)
    for q4 in range(4):
        si = kb.i_wst
        kb.i_wst = (si + 1) % kb.NST
        st, t_st = kb.wst[si], kb.t_wst[si]
        P.dins("sp", "dma_start", dict(out=st[:, 0:1024], in_=mexp[:, q4 * 1024:(q4 + 1) * 1024]), writes=[t_st])
        P.ins("pool", "tensor_copy", dict(out=mxb[:, q4 * 1024:(q4 + 1) * 1024], in_=st[:, 0:1024]), reads=[t_st], writes=[t_mxb])
    es, t_es = kb.mtemp("es", 16, 1)
    P.ins("act", "activation", dict(out=es, in_=cst[:, C_SINK:C_SINK + 16], func=AF.Exp), reads=[kb.t_cst], writes=[t_es])
    kn, t_kn = kb.mtemp("kn", (NB + 1) * 128 // 2, 1, BF16)
    vt0, t_vt = kb.mtemp("vtmS", (NB + 1) * 66 // 2, 1, BF16)
    vt = vt0.rearrange("p (b d) -> p b d", d=66)
    P.ins("dve", "memset", dict(ap=vt0, constant=1.0), writes=[t_vt])
    t_knb = P.tks(NB + 1, "knb")
    t_vtb = P.tks(NB + 1, "vtb")

    def qk_norm(pk, t_pk, out_ap, t_out, gc, cols):
        sq, t_sq = kb.mtemp("sq", TB // 2, 2, BF16)
        P.ins("act", "activation", dict(out=sq[:, :cols], in_=pk[:, :cols], func=AF.Square), reads=[t_pk], writes=[t_sq])
        pss, t_pss = kb.sbank()
        kb.mm(pss[:, :cols], t_pss, kb.bones, sq[:, :cols], [t_sq, kb.t_cmat], True, True)
        rs, t_rs = kb.rstd_from(pss, t_pss, 64, cols=cols)
        P.ins("dve", "scalar_tensor_tensor", dict(out=out_ap, in0=pk[:, :cols], scalar=cst[:, gc:gc + 1], in1=rs[:, :cols], op0=ALU.mult, op1=ALU.mult),
              reads=[t_pk, t_rs, kb.t_cst], writes=t_out)

    for g in range(4):
        wq = [kb.load_w(W[:, g * 256 + sl * 128:g * 256 + (sl + 1) * 128], KC) for sl in range(2)]
        wk = kb.load_w(W[:, 1024 + g * 64:1024 + (g + 1) * 64], KC, ncols=64, dup=True)
        wv = kb.load_w(W[:, 1280 + g * 64:1280 + (g + 1) * 64], KC, ncols=64)
        pk, t_pk = kb.sbank()
        for k in range(KC):
            kb.mm(pk[:, :128], t_pk, wk[0][:, k, :], hxn[:, k, :], [wk[1], t_hxn], k == 0, k == KC - 1)
        qk_norm(pk, t_pk, kn[:, 0:128], [t_knb[0]], C_KG, 128)
        pv, t_pv = kb.sbank()
        for k in range(KC):
            kb.mm(pv[:, :64], t_pv, hxn[:, k, :], wv[0][:, k, :], [wv[1], t_hxn], k == 0, k == KC - 1)
        P.ins("act", "activation", dict(out=vt[:, 0, 0:64], in_=pv[:, 0:64], func=AF.Copy), reads=[t_pv], writes=[t_vtb[0]])
        for tb in range(NTB):
            ts = slice(tb * TB, (tb + 1) * TB)
            rx = lambda k: [t_xn[k][tb]]
            pk, t_pk = kb.sbank()
            for k in range(KC):
                kb.mm(pk, t_pk, wk[0][:, k, :], xn[:, k, ts], [wk[1]] + rx(k), k == 0, k == KC - 1)
            qk_norm(pk, t_pk, kn[:, 128 + tb * TB:128 + (tb + 1) * TB], [t_knb[1 + tb * 4 + i] for i in range(4)], C_KG, TB)
            for blk in range(4):
                b = tb * 4 + blk
                pv, t_pv = kb.sbank()
                t0 = b * 128
                for k in range(KC):
                    kb.mm(pv[:, :64], t_pv, xn[:, k, t0:t0 + 128], wv[0][:, k, :], [wv[1]] + rx(k), k == 0, k == KC - 1)
                P.ins("act", "activation", dict(out=vt[:, 1 + b, 0:64], in_=pv[:, 0:64], func=AF.Copy), reads=[t_pv], writes=[t_vtb[1 + b]])
            qn = []
            for sl in range(2):
                pq, t_pq = kb.sbank()
                for k in range(KC):
                    kb.mm(pq, t_pq, wq[sl][0][:, k, :], xn[:, k, ts], [wq[sl][1]] + rx(k), k == 0, k == KC - 1)
                q_, t_q = kb.mtemp("qn", TB // 2, 4, BF16)
                qk_norm(pq, t_pq, q_, [t_q], C_QG, TB)
                qn.append((q_, t_q))
            for blk in range(4):
                b = tb * 4 + blk
                bs = slice(blk * 128, (blk + 1) * 128)
                otm, t_otm = kb.mtemp("otm", 256 // 2, 2, BF16)
                for j in range(4):
                    h = 4 * g + j
                    sl = j // 2
                    hp = slice((j % 2) * 64, (j % 2) * 64 + 64)
                    psc, t_psc = kb.sbank()
                    kb.mm(psc[:, 0:128], t_psc, kn[hp, b * 128:(b + 1) * 128], qn[sl][0][hp, bs], [t_knb[b], qn[sl][1]], True, True)
                    kb.mm(psc[:, 128:256], t_psc, kn[hp, (b + 1) * 128:(b + 2) * 128], qn[sl][0][hp, bs], [t_knb[b + 1], qn[sl][1]], True, True)
                    pex, t_pex = kb.mtemp("pex", 256, 2)
                    P.ins("act", "activation", dict(out=pex, in_=psc[:, 0:256], func=AF.Exp, scale=0.125), reads=[t_psc], writes=[t_pex])
                    pm, t_pm = kb.mtemp("pm", 256 // 2, 2, BF16)
                    P.ins("dve", "tensor_tensor", dict(out=pm, in0=pex, in1=mxb[:, h * 256:(h + 1) * 256], op=ALU.mult), reads=[t_pex, t_mxb], writes=[t_pm])
                    if b == 0:
                        P.ins("dve", "tensor_scalar", dict(out=pm[:, 0:128], in0=pm[:, 0:128], scalar1=cst[:, C_FLAG:C_FLAG + 1], scalar2=None, op0=ALU.mult),
                              reads=[kb.t_cst], writes=[t_pm])
                    po, t_po = kb.sbank()
                    kb.mm(po[:, :65], t_po, pm[:, 0:128], vt[:, b, 0:65], [t_pm, t_vtb[b]], True, False)
                    kb.mm(po[:, :65], t_po, pm[:, 128:256], vt[:, b + 1, 0:65], [t_pm, t_vtb[b + 1]], False, True)
                    dn, t_dn = kb.mtemp("dn", 2, 2)
                    P.ins("dve", "tensor_tensor", dict(out=dn[:, 0:1], in0=po[:, 64:65], in1=es[:, h:h + 1], op=ALU.add), reads=[t_po, t_es], writes=[t_dn])
                    P.ins("dve", "reciprocal", dict(out=dn[:, 0:1], in_=dn[:, 0:1]), writes=[t_dn])
                    P.ins("dve", "tensor_scalar", dict(out=otm[:, j * 64:(j + 1) * 64], in0=po[:, 0:64], scalar1=dn[:, 0:1], scalar2=None, op0=ALU.mult),
                          reads=[t_po, t_dn], writes=[t_otm])
                ptr, t_ptr = kb.sbank()
                ptrb = ptr[:, 0:128].bitcast(BF16)
                for i2 in range(2):
                    P.ins("pe", "transpose", dict(out=ptrb[:, i2 * 128:(i2 + 1) * 128], in_=otm[:, i2 * 128:(i2 + 1) * 128], identity=kb.ident),
                          reads=[t_otm, kb.t_cmat], writes=[t_ptr])
                for i2 in range(2):
                    P.ins("act", "activation", dict(out=kb.R[:, 2 * g + i2, b * 128:(b + 1) * 128], in_=ptrb[:, i2 * 128:(i2 + 1) * 128], func=AF.Copy),
                          reads=[t_ptr], writes=[kb.t_R[2 * g + i2][tb]])


def _mix_weights(inp, l):
    kind = KINDS[l]
    j = l // 3
    if kind == "hg":
        return inp["hg_w_in"][j], inp["hg_w_out"][j]
    if kind == "ml":
        return inp["ml_w_qkvo"][j], inp["ml_w_out"][j]
    return inp["sw_w_qkv"][j], inp["sw_w_o"][j]


def kernel(**inputs):
    inp = {k: np.ascontiguousarray(np.asarray(v, dtype=np.float32)) for k, v in inputs.items()}
    x = inp["x"][0]
    cm = make_cmat()
    csts = [make_cst(inp, c) for c in range(NCORES)]
    mexp = make_mexp()
    cur = [np.ascontiguousarray(x[c * NT:(c + 1) * NT].T) for c in range(NCORES)]
    extra = [dict() for _ in range(NCORES)]
    for step in range(DEPTH + 1):
        lB = step - 1 if step > 0 else None
        lA = step if step < DEPTH else None
        nc, _ = build_launch(lB, lA)
        maps = []
        for c in range(NCORES):
            m = {"xin": cur[c], "cst": csts[c], "cmat": cm}
            if lB is not None:
                wmix, wo = _mix_weights(inp, lB)
                m.update({"wmixB": wmix, "woB": wo, "wguB": inp["w_ffn_gu"][lB, 1], "wdnB": inp["w_ffn_down"][lB, 1],
                          "wplg": inp["w_ple_gate"][lB], "wplp": inp["w_ple_proj"][lB],
                          "pT": np.ascontiguousarray(inp["p"][lB, 0, c * NT:(c + 1) * NT].T)})
                m.update(extra[c])
                if KINDS[lB] == "sw":
                    m["mexp"] = mexp
            if lA is not None:
                m.update({"wguA": inp["w_ffn_gu"][lA, 0], "wdnA": inp["w_ffn_down"][lA, 0]})
                if KINDS[lA] in GLA_H:
                    m["wmixA"] = _mix_weights(inp, lA)[0]
            maps.append(m)
        res = run_bass_kernel_spmd(nc, maps, core_ids=list(range(NCORES))).results
        cur = [np.ascontiguousarray(res[c]["xout"]) for c in range(NCORES)]
        extra = [dict() for _ in range(NCORES)]
        if lA is not None:
            if KINDS[lA] in GLA_H:
                sS = np.ascontiguousarray(np.stack([res[c]["sendS"] for c in range(NCORES)], 1))
                sD = np.stack([res[c]["sendD"] for c in range(NCORES)], 2)
                gD = np.ascontiguousarray(sD.transpose(1, 0, 2))
                for c in range(NCORES):
                    extra[c] = {"gS": sS, "gD": gD}
            else:
                for c in range(NCORES):
                    extra[c] = {"halo": np.ascontiguousarray(cur[c - 1][:, NT - 128:]) if c > 0 else np.zeros((D, 128), np.float32)}
    out = np.concatenate([cur[c].T for c in range(NCORES)], axis=0)
    return np.ascontiguousarray(out.reshape(1, SEQ, D).astype(np.float32))
```
